# Optimizing a Trainium2 kernel written in Bass

```python
import jax, jax.numpy as jnp
from jax import lax
import numpy as np

D_MODEL = 1024
BATCH = 8
SEQ = 8192
DEPTH = 4

CHUNK = 64
D_MIX = D_MODEL
D_CONV = D_MIX // 4
D_POOL = D_MIX // 4
D_ATTN = D_MIX // 2
HEAD_DIM = 64
N_HEADS = D_ATTN // HEAD_DIM
CONV_W = 3
POOL_WINDOWS = (2, 4, 8, 16)
N_POOL = len(POOL_WINDOWS)
POOL_GC = D_POOL // N_POOL
LEFT_CHUNKS = 8
BAND = (LEFT_CHUNKS + 1) * CHUNK
REL_CLIP = 128
D_FF = ((8 * D_MODEL // 3 + 255) // 256) * 256
IN_COLS = 3 * D_CONV + D_POOL + 3 * D_ATTN
EPS = 1e-6

kernel_name = "hybrid_conv_pool_chunkattn_trunk"


def rmsnorm(x, g):
    xf = x.astype(jnp.float32)
    y = xf * lax.rsqrt(jnp.mean(xf * xf, axis=-1, keepdims=True) + EPS)
    return (y * g.astype(jnp.float32)).astype(x.dtype)


def shift_right(u, n):
    s = u.shape[1]
    return jnp.pad(u, ((0, 0), (n, 0), (0, 0)))[:, :s]


def short_conv_mixer(gb, gc, u, conv_w):
    z = gc * u
    conv = conv_w[2] * z + conv_w[1] * shift_right(z, 1) + conv_w[0] * shift_right(z, 2)
    return gb * conv


def pool_mixer(u, pool_w, pool_scale):
    b, s, _ = u.shape
    uf = u.astype(jnp.float32)
    cs = jnp.cumsum(uf, axis=1)
    t = jnp.arange(s)
    outs = []
    for gi, w in enumerate(POOL_WINDOWS):
        sl = slice(gi * POOL_GC, (gi + 1) * POOL_GC)
        c = cs[..., sl]
        cnt = jnp.minimum(t + 1, w).astype(jnp.float32)[None, :, None]
        outs.append((c - shift_right(c, w)) / cnt - uf[..., sl])
    d = jnp.concatenate(outs, axis=-1).astype(u.dtype).reshape(b, s, N_POOL, POOL_GC)
    y = jnp.einsum('bsgc,gcd->bsgd', d, pool_w).reshape(b, s, D_POOL)
    return y * pool_scale


def chunk_attention(q, k, v, rel_bias):
    b, s, h, dh = q.shape
    n_chunks = s // CHUNK
    pad = ((0, 0), (LEFT_CHUNKS * CHUNK, 0), (0, 0), (0, 0))
    kpad = jnp.pad(k, pad)
    vpad = jnp.pad(v, pad)
    qi = jnp.arange(CHUNK)[:, None]
    kj = jnp.arange(BAND)[None, :]
    rel = LEFT_CHUNKS * CHUNK + qi - kj
    idx = jnp.clip(rel, -REL_CLIP, REL_CLIP) + REL_CLIP
    bias = rel_bias[:, idx].astype(jnp.float32)
    key_off = jnp.arange(BAND)
    scale = HEAD_DIM ** -0.5

    def one_chunk(c):
        qc = lax.dynamic_slice_in_dim(q, c * CHUNK, CHUNK, axis=1)
        kb = lax.dynamic_slice_in_dim(kpad, c * CHUNK, BAND, axis=1)
        vb = lax.dynamic_slice_in_dim(vpad, c * CHUNK, BAND, axis=1)
        sc = jnp.einsum('bqhd,bkhd->bhqk', qc, kb).astype(jnp.float32) * scale + bias[None]
        valid = key_off >= (LEFT_CHUNKS - c) * CHUNK
        sc = jnp.where(valid, sc, jnp.finfo(jnp.float32).min)
        p = jax.nn.softmax(sc, axis=-1).astype(vb.dtype)
        return jnp.einsum('bhqk,bkhd->bqhd', p, vb)

    out = lax.map(one_chunk, jnp.arange(n_chunks))
    return out.transpose(1, 0, 2, 3, 4).reshape(b, s, h * dh)


def setup_inputs(seed: int = 0) -> dict:
    key = jax.random.key(seed)
    ks = jax.random.split(key, 16)
    f32 = jnp.float32
    nrm = lambda k, shp, sc: jax.random.normal(k, shp, f32) * sc
    gain = lambda k, shp: 1.0 + 0.05 * jax.random.normal(k, shp, f32)
    return {
        "x": jax.random.normal(ks[0], (BATCH, SEQ, D_MODEL), f32),
        "w_in": nrm(ks[1], (DEPTH, D_MODEL, IN_COLS), D_MODEL ** -0.5),
        "w_out": nrm(ks[2], (DEPTH, D_MIX, D_MODEL), D_MIX ** -0.5),
        "conv_w": nrm(ks[3], (DEPTH, CONV_W, D_CONV), CONV_W ** -0.5),
        "pool_w": nrm(ks[4], (DEPTH, N_POOL, POOL_GC, POOL_GC), POOL_GC ** -0.5),
        "pool_scale": gain(ks[5], (DEPTH, D_POOL)),
        "rel_bias": nrm(ks[6], (DEPTH, N_HEADS, 2 * REL_CLIP + 1), 0.1),
        "group_gain": gain(ks[7], (DEPTH, D_MIX)),
        "pre_mix_g": gain(ks[8], (DEPTH, D_MODEL)),
        "post_mix_g": gain(ks[9], (DEPTH, D_MODEL)),
        "pre_ffn_g": gain(ks[10], (DEPTH, D_MODEL)),
        "post_ffn_g": gain(ks[11], (DEPTH, D_MODEL)),
        "w_gate_up": nrm(ks[12], (DEPTH, D_MODEL, 2 * D_FF), D_MODEL ** -0.5),
        "w_down": nrm(ks[13], (DEPTH, D_FF, D_MODEL), D_FF ** -0.5),
    }


def reference(x, w_in, w_out, conv_w, pool_w, pool_scale, rel_bias, group_gain,
              pre_mix_g, post_mix_g, pre_ffn_g, post_ffn_g, w_gate_up, w_down):
    b, s, _ = x.shape
    h = x
    for l in range(DEPTH):
        xn = rmsnorm(h, pre_mix_g[l])
        proj = jnp.einsum('bsd,dc->bsc', xn, w_in[l])
        o = 0
        gb = proj[..., o:o + D_CONV]; o += D_CONV
        gc = proj[..., o:o + D_CONV]; o += D_CONV
        u = proj[..., o:o + D_CONV]; o += D_CONV
        pu = proj[..., o:o + D_POOL]; o += D_POOL
        q = proj[..., o:o + D_ATTN].reshape(b, s, N_HEADS, HEAD_DIM); o += D_ATTN
        k = proj[..., o:o + D_ATTN].reshape(b, s, N_HEADS, HEAD_DIM); o += D_ATTN
        v = proj[..., o:o + D_ATTN].reshape(b, s, N_HEADS, HEAD_DIM)

        ya = short_conv_mixer(gb, gc, u, conv_w[l])
        yb = pool_mixer(pu, pool_w[l], pool_scale[l])
        yc = chunk_attention(q, k, v, rel_bias[l])

        gg = group_gain[l]
        ya = rmsnorm(ya, gg[:D_CONV])
        yb = rmsnorm(yb, gg[D_CONV:D_CONV + D_POOL])
        yc = rmsnorm(yc, gg[D_CONV + D_POOL:])
        y = jnp.concatenate([ya, yb, yc], axis=-1)
        mix = jnp.einsum('bsc,cd->bsd', y, w_out[l])
        h = h + rmsnorm(mix, post_mix_g[l])

        hn = rmsnorm(h, pre_ffn_g[l])
        gu = jnp.einsum('bsd,df->bsf', hn, w_gate_up[l])
        ff = jax.nn.silu(gu[..., :D_FF]) * gu[..., D_FF:]
        ffo = jnp.einsum('bsf,fd->bsd', ff, w_down[l])
        h = h + rmsnorm(ffo, post_ffn_g[l])
    return h
```

```python
import numpy as np
from contextlib import ExitStack
import concourse.bass as bass
import concourse.mybir as mybir
from concourse.bass_utils import run_bass_kernel_spmd

F32 = mybir.dt.float32
BF16 = mybir.dt.bfloat16
ALU = mybir.AluOpType
AF = mybir.ActivationFunctionType

D = 1024
SEQ = 8192
TB = 512
KT = 8
DEPTH = 4
DFF = 2816
FT = 22
NHEAD = 8
EPS = 1e-6
NEG = -30000.0

OFF_V = 0
OFF_WIN = 4096
OFF_WOUT = OFF_WIN + 16 * 1024
OFF_GU = OFF_WOUT + 8 * 1024
OFF_DN = OFF_GU + 44 * 1024
WROW = OFF_DN + 8 * 2816
NPRM = 56
P_PREMIX, P_POSTMIX, P_PREFFN, P_POSTFFN, P_GG, P_CONV, P_PSCALE, P_CH = 0, 8, 16, 24, 32, 40, 46, 48

ENGS = ("pe", "act", "dve", "pool", "sp")


class Op:
    __slots__ = ("eng", "fn", "dma", "dma_val", "deps", "marked", "val", "waits")

    def __init__(self, eng, fn, dma):
        self.eng = eng
        self.fn = fn
        self.dma = dma
        self.dma_val = 0
        self.deps = ()
        self.marked = False
        self.val = 0
        self.waits = ()


class Sched:
    def __init__(self):
        self.ops = {e: [] for e in ENGS}
        self.lastw = {}
        self.readers = {}
        self.dma_count = {}

    def add(self, eng, fn, reads=(), writes=(), dma=None):
        op = Op(eng, fn, dma)
        deps = set()
        for r in reads:
            w = self.lastw.get(r)
            if w is not None:
                deps.add(w)
        for r in writes:
            w = self.lastw.get(r)
            if w is not None:
                deps.add(w)
            for rd in self.readers.get(r, ()):
                deps.add(rd)
        op.deps = deps
        for r in reads:
            self.readers.setdefault(r, []).append(op)
        for r in writes:
            self.lastw[r] = op
            self.readers[r] = []
        if dma is not None:
            c = self.dma_count.get(dma, 0) + 1
            self.dma_count[dma] = c
            op.dma_val = 16 * c
        self.ops[eng].append(op)
        return op

    def finalize(self):
        for eng in ENGS:
            for op in self.ops[eng]:
                for d in op.deps:
                    if d.dma is None and not (eng == "pe" and d.eng == "pe"):
                        d.marked = True
        for eng in ENGS:
            cnt = 0
            for op in self.ops[eng]:
                if op.marked:
                    cnt += 1
                    op.val = cnt
        for eng in ENGS:
            seen = {}
            for op in self.ops[eng]:
                need = {}
                for d in op.deps:
                    if d.dma is not None:
                        key = ("dma", d.dma)
                        v = d.dma_val
                    else:
                        if eng == "pe" and d.eng == "pe":
                            continue
                        key = ("eng", d.eng)
                        v = d.val
                    if v > need.get(key, 0):
                        need[key] = v
                w = []
                for key, v in need.items():
                    if v > seen.get(key, 0):
                        seen[key] = v
                        w.append((key, v))
                op.waits = w
                op.deps = ()


def build_program(NB=16, layers=(0, 1, 2, 3), do_prologue=True):
    S = NB * TB
    NL = len(layers)
    nc = bass.Bass("TRN2", target_bir_lowering=False)
    xT = nc.dram_tensor("xT", [D, S], F32, kind="ExternalInput").ap()
    wflat = nc.dram_tensor("wflat", [NL, 128, WROW], F32, kind="ExternalInput").ap()
    prm_d = nc.dram_tensor("prm", [NL, 128, NPRM], F32, kind="ExternalInput").ap()
    tbl_d = nc.dram_tensor("tbl", [NL, 128, NHEAD * 256], F32, kind="ExternalInput").ap()
    poolw_d = nc.dram_tensor("poolw", [NL, 128, 256], F32, kind="ExternalInput").ap()
    cst_d = nc.dram_tensor("cst", [128, 128 + 32], F32, kind="ExternalInput").ap()
    yT = nc.dram_tensor("yT", [D, S], F32, kind="ExternalOutput").ap()
    w16 = nc.dram_tensor("w16", [NL, 128, WROW], BF16, kind="Internal").ap()

    xT_v = xT.rearrange("(kt p) n -> p kt n", p=128)
    yT_v = yT.rearrange("(kt p) n -> p kt n", p=128)

    sch = Sched()
    es = ExitStack()
    with es:
        def sb(name, shape, dt):
            return es.enter_context(nc.sbuf_tensor(name, shape, dt))

        hT = sb("hT", [128, KT, TB], F32)
        NSLOT = 5
        KTs = [sb(f"KTs{i}", [128, 4, TB], BF16) for i in range(NSLOT)]
        Vs = [sb(f"Vs{i}", [128, 4, 512], BF16) for i in range(NSLOT)]
        tbl8 = sb("tbl8", [128, NL, NHEAD, 256], BF16)
        prm = sb("prm_sb", [128, NL, NPRM], F32)
        poolW = sb("poolW", [128, NL, 256], BF16)
        cst = sb("cst_sb", [128, 160], F32)
        ident = sb("ident", [128, 128], BF16)
        ones = sb("ones", [128, 128], BF16)
        negrow = sb("negrow", [1, 128], BF16)
        epsT = sb("epsT", [128, 1], F32)
        maskcol = sb("maskcol", [128, 1], F32)
        dummy = sb("dummyln", [128, 2], F32)
        zc = sb("zc", [128, NL, 2, 2], F32)
        pc = sb("pc", [128, NL, 2, 16], F32)
        xn = sb("xn", [128, KT, TB], BF16)
        sq = sb("sq", [128, 4, TB], BF16)
        sqc = [0]
        rs_tmp = sb("rs_tmp", [128, TB], F32)
        rstd = sb("rstd", [128, 2, TB], F32)
        tmpA = sb("tmpA", [128, 2, TB], F32)
        tmpP = sb("tmpP", [128, 2, TB], F32)
        PT = sb("PT", [128, 6, TB], BF16)
        sqacc = sb("sqacc", [128, TB], F32)
        sqat = sb("sqat", [128, TB], BF16)
        scrA = sb("scrA", [128, 6184], F32)
        mix = scrA[:, 0:4096].rearrange("p (k n) -> p k n", k=KT)
        gb_sb = scrA[:, 0:1024].rearrange("p (k n) -> p k n", k=2)
        gc_sb = scrA[:, 1024:2048].rearrange("p (k n) -> p k n", k=2)
        zbuf = scrA[:, 2048:2048 + 2 * 514].rearrange("p (k n) -> p k n", k=2)
        ya = scrA[:, 3080:3080 + 1024].rearrange("p (k n) -> p k n", k=2)
        pu_sb = scrA[:, 4104:4104 + 2 * 528].rearrange("p (k n) -> p k n", k=2)
        ybuf = scrA[:, 5160:5160 + 1024].rearrange("p (k n) -> p k n", k=2)
        scrB = sb("scrB", [128, 5632], F32)
        actT = scrB[:, :].bitcast(BF16).rearrange("p (k n) -> p k n", k=FT)
        qT = scrB[:, 0:1024].bitcast(BF16).rearrange("p (k n) -> p k n", k=4)
        yTb = scrB[:, 1024:3072].bitcast(BF16).rearrange("p (k n) -> p k n", k=8)
        onb = scrB[:, 3072:5120].rearrange("p (k n) -> p k n", k=4)
        wsT = sb("wsT", [128, 3, 528], F32)
        dbf = sb("dbf", [128, 2, TB], BF16)
        NRA = 4
        ringA = sb("ringA", [128, NRA, 2048], BF16)
        NRB = 2
        ringB = sb("ringB", [128, NRB, 2816], BF16)
        psA = es.enter_context(nc.psum_tensor("psA", [128, 8 * 512], F32))
        ps = [psA[:, i * 512:(i + 1) * 512] for i in range(8)]

        def R_mix(k):
            return ("scrA", k)

        def scrA_keys(lo, hi):
            return [("scrA", i) for i in range(lo // 512, (hi + 511) // 512)]

        def scrB_keys(lo, hi):
            return [("scrB", i) for i in range(lo // 256, (hi + 255) // 256)]

        K_gb = [scrA_keys(0 + t * 512, 512 + t * 512) for t in range(2)]
        K_gc = [scrA_keys(1024 + t * 512, 1536 + t * 512) for t in range(2)]
        K_z = [scrA_keys(2048 + t * 514, 2048 + (t + 1) * 514) for t in range(2)]
        K_ya = [scrA_keys(3080 + t * 512, 3080 + (t + 1) * 512) for t in range(2)]
        K_pu = [scrA_keys(4104 + t * 528, 4104 + (t + 1) * 528) for t in range(2)]
        K_yb = [scrA_keys(5160 + t * 512, 5160 + (t + 1) * 512) for t in range(2)]
        K_mix = [scrA_keys(k * 512, (k + 1) * 512) for k in range(KT)]
        K_act = [scrB_keys(f * 256, (f + 1) * 256) for f in range(FT)]
        K_q = [scrB_keys(t * 256, (t + 1) * 256) for t in range(4)]
        K_y = [scrB_keys(1024 + t * 256, 1024 + (t + 1) * 256) for t in range(8)]
        K_on = [scrB_keys(3072 + t * 512, 3072 + (t + 1) * 512) for t in range(4)]

        sem_eng = {e: es.enter_context(nc.semaphore(f"sem_{e}")) for e in ("pe", "act", "dve", "pool")}
        dma_keys = [f"ra{i}" for i in range(NRA)] + [f"rb{i}" for i in range(NRB)] + \
                   ["hld0", "hld1", "yst0", "yst1", "misc0", "misc1", "stin0", "stin1", "stout0", "stout1"] + [f"cv{i}_{c}" for i in range(NL) for c in range(8)]
        sem_dma = {k: es.enter_context(nc.semaphore(f"semd_{k}")) for k in dma_keys}

        A = sch.add

        A("sp", lambda e: e.dma_start(out=prm[:], in_=prm_d.rearrange("l p c -> p l c")),
          writes=["prm"], dma="misc0")
        A("sp", lambda e: e.dma_start(out=cst[:], in_=cst_d), writes=["cst"], dma="misc1")
        A("dve", lambda e: e.tensor_copy(out=ident[:], in_=cst[:, 0:128]), reads=["cst"], writes=["ident"])
        A("dve", lambda e: e.memset(ones[:], 1.0), writes=["ones"])
        A("dve", lambda e: e.memset(negrow[:], 0.0), writes=["negrow"])
        A("dve", lambda e: e.memset(negrow[0:1, 0:64], NEG), reads=[], writes=["negrow"])
        A("dve", lambda e: e.memset(epsT[:], EPS), writes=["epsT"])
        A("dve", lambda e: e.memset(maskcol[:], 0.0), writes=["maskcol"])
        A("dve", lambda e: e.memset(maskcol[0:64, :], NEG), writes=["maskcol"])

        st_in = [hT[:, :, :].rearrange("p k n -> p (k n)"), scrA[:, 0:4096]]
        st_in_keys = [[("h", k) for k in range(KT)], [k for kk in K_mix for k in kk]]
        st_out = [scrB[:, 0:2048].bitcast(BF16), scrB[:, 2048:4096].bitcast(BF16)]
        st_out_keys = [scrB_keys(0, 2048), scrB_keys(2048, 4096)]

        for li in range(NL):
            si = li % 2
            A("sp", lambda e, li=li, si=si: e.dma_start(out=st_in[si][:, 0:2048], in_=tbl_d[li]),
              writes=st_in_keys[si], dma=f"stin{si}")
            for h in range(NHEAD):
                A("dve", lambda e, li=li, si=si, h=h: e.tensor_scalar(
                    out=tbl8[:, li, h, :], in0=st_in[si][:, h * 256:(h + 1) * 256],
                    scalar1=prm[:, li, P_CH + h:P_CH + h + 1], scalar2=8.0,
                    op0=ALU.subtract, op1=ALU.mult),
                  reads=st_in_keys[si] + ["prm"], writes=[("tbl8", li)])
            A("dve", lambda e, li=li: e.memset(tbl8[64:128, li, :, 0:64], NEG), writes=[("tbl8", li)])
        A("sp", lambda e: e.dma_start(out=st_in[0][:, 0:NL * 256].rearrange("p (l c) -> p l c", l=NL),
                                      in_=poolw_d.rearrange("l p c -> p l c")),
          writes=st_in_keys[0], dma="stin0")
        A("dve", lambda e: e.tensor_copy(out=poolW[:].rearrange("p l c -> p (l c)"), in_=st_in[0][:, 0:NL * 256]),
          reads=st_in_keys[0], writes=["poolW"])

        NCV = 8
        cstep = WROW // NCV
        for li in range(NL):
            for c in range(NCV):
                rd = [("cvdone", li - 1, cc) for cc in range(NCV)] if li > 0 else []
                A("pool", lambda e, li=li, c=c: e.dma_start(out=w16[li, :, c * cstep:(c + 1) * cstep],
                                                            in_=wflat[li, :, c * cstep:(c + 1) * cstep]),
                  reads=rd, writes=[("w16", li, c), ("cvdone", li, c)], dma=f"cv{li}_{c}")

        loads = []
        for b in range(NB):
            for li in range(NL):
                for u in range(36):
                    loads.append(("A", li, u * 2048))
                for m in range(8):
                    loads.append(("B", li, OFF_DN + m * 2816))
        state = {"nextA": 0, "nextB": 0, "issuedA": 0, "issuedB": 0}
        loadsA = [x for x in loads if x[0] == "A"]
        loadsB = [x for x in loads if x[0] == "B"]

        def issue_A():
            i = state["issuedA"]
            if i >= len(loadsA):
                return
            _, li, off = loadsA[i]
            slot = i % NRA
            A("sp", lambda e, li=li, off=off, slot=slot: e.dma_start(out=ringA[:, slot, :], in_=w16[li, :, off:off + 2048]),
              reads=[("w16", li, c) for c in range(off // cstep, min(NCV - 1, (off + 2047) // cstep) + 1)], writes=[("ra", slot)], dma=f"ra{slot}")
            state["issuedA"] = i + 1

        def issue_B():
            i = state["issuedB"]
            if i >= len(loadsB):
                return
            _, li, off = loadsB[i]
            slot = i % NRB
            A("sp", lambda e, li=li, off=off, slot=slot: e.dma_start(out=ringB[:, slot, :], in_=w16[li, :, off:off + 2816]),
              reads=[("w16", li, c) for c in range(off // cstep, min(NCV - 1, (off + 2815) // cstep) + 1)], writes=[("rb", slot)], dma=f"rb{slot}")
            state["issuedB"] = i + 1

        for _ in range(NRA):
            issue_A()
        for _ in range(NRB):
            issue_B()

        def take_A():
            i = state["nextA"]
            state["nextA"] = i + 1
            return i % NRA

        def take_B():
            i = state["nextB"]
            state["nextB"] = i + 1
            return i % NRB

        def rstd_from(bank, nfeat, rs_idx):
            A("act", lambda e, bank=bank, nfeat=nfeat: e.activation(out=rs_tmp[:, :], in_=ps[bank][:, :], func=AF.Ln,
                                                                    bias=epsT[:, 0:1], scale=1.0 / nfeat),
              reads=[("ps", bank), "epsT"], writes=["rs_tmp"])
            A("act", lambda e, rs_idx=rs_idx: e.activation(out=rstd[:, rs_idx, :], in_=rs_tmp[:, :], func=AF.Exp,
                                                            scale=-0.5),
              reads=["rs_tmp"], writes=[("rstd", rs_idx)])

        def sq_stat(ap, keys, i, n, bank, sq_eng="act"):
            sb_i = sqc[0] % 4
            sqc[0] += 1
            if sq_eng == "act":
                A("act", lambda e: e.activation(out=sq[:, sb_i, :], in_=ap, func=AF.Square),
                  reads=keys, writes=[("sq", sb_i)])
            else:
                A(sq_eng, lambda e: e.tensor_tensor(out=sq[:, sb_i, :], in0=ap, in1=ap, op=ALU.mult),
                  reads=keys, writes=[("sq", sb_i)])

            def mm():
                A("pe", lambda e: e.matmul(ps[bank][:, :], ones[:, :], sq[:, sb_i, :],
                                           start=(i == 0), stop=(i == n - 1)),
                  reads=[("sq", sb_i), "ones"], writes=[("ps", bank)])
            return mm

        def rms_stats(src_tiles, src_keys, nfeat, rs_idx, sq_eng="act", bank=7):
            n = len(src_tiles)
            for i, (ap, keys) in enumerate(zip(src_tiles, src_keys)):
                sq_stat(ap, keys, i, n, bank, sq_eng)()
            rstd_from(bank, nfeat, rs_idx)

        def pre_norm(li, pcol):
            rms_stats([hT[:, k, :] for k in range(KT)], [[("h", k)] for k in range(KT)], D, 0)
            for k in range(KT):
                A("dve", lambda e, k=k, li=li, pcol=pcol: e.scalar_tensor_tensor(
                    out=xn[:, k, :], in0=hT[:, k, :], scalar=prm[:, li, pcol + k:pcol + k + 1],
                    in1=rstd[:, 0, :], op0=ALU.mult, op1=ALU.mult),
                  reads=[("h", k), ("rstd", 0), "prm"], writes=[("xn", k)])

        def post_norm_residual(li, pcol):
            rstd_from(7, D, 0)
            for k in range(KT):
                ti = k % 2
                A("dve", lambda e, k=k, ti=ti, li=li, pcol=pcol: e.scalar_tensor_tensor(
                    out=tmpA[:, ti, :], in0=mix[:, k, :], scalar=prm[:, li, pcol + k:pcol + k + 1],
                    in1=rstd[:, 0, :], op0=ALU.mult, op1=ALU.mult),
                  reads=K_mix[k] + [("rstd", 0), "prm"], writes=[("tmpA", ti)])
                A("dve",
                  lambda e, k=k, ti=ti: e.tensor_tensor(out=hT[:, k, :], in0=hT[:, k, :], in1=tmpA[:, ti, :],
                                                        op=ALU.add),
                  reads=[("h", k), ("tmpA", ti)], writes=[("h", k)])

        def proj_chunks(n_chunks, rhs_tiles, rhs_keys, nk, evac, banks, ring="A"):
            bi = 0
            m = 0
            deferred = []
            while m < n_chunks:
                if ring == "A":
                    slot = take_A()
                    wk = ("ra", slot)
                    subs = [0, 1]
                else:
                    slot = take_B()
                    wk = ("rb", slot)
                    subs = [0]
                for sidx in subs:
                    bank = banks[bi % len(banks)]
                    bi += 1
                    for k in range(nk):
                        if ring == "A":
                            lhsT = ringA[:, slot, sidx * 1024 + k * 128: sidx * 1024 + (k + 1) * 128]
                        else:
                            lhsT = ringB[:, slot, k * 128:(k + 1) * 128]
                        A("pe", lambda e, bank=bank, lhsT=lhsT, k=k, nk=nk: e.matmul(
                            ps[bank][:, :], lhsT, rhs_tiles[k], start=(k == 0), stop=(k == nk - 1)),
                          reads=[wk] + rhs_keys[k], writes=[("ps", bank)])
                    dfr = evac(m, bank)
                    if dfr is not None:
                        deferred.append(dfr)
                    if len(deferred) > 2:
                        deferred.pop(0)()
                    m += 1
                if ring == "A":
                    issue_A()
                else:
                    issue_B()
            for dfr in deferred:
                dfr()

        def evac_branch(m, bank):
            dfr = sq_stat(ps[bank][:, :], [("ps", bank)], m, KT, 7, "act")
            A("act", lambda e, m=m, bank=bank: e.activation(out=mix[:, m, :], in_=ps[bank][:, :], func=AF.Copy),
              reads=[("ps", bank)], writes=K_mix[m])
            return dfr

        free_slots = list(range(NSLOT))
        prev_slot = {li: None for li in range(NL)}

        for b in range(NB):
            for hh in range(2):
                A("sp", lambda e, b=b, hh=hh: e.dma_start(out=hT[:, 4 * hh:4 * hh + 4, :],
                                                          in_=xT_v[:, 4 * hh:4 * hh + 4, b * TB:(b + 1) * TB]),
                  writes=[("h", k) for k in range(4 * hh, 4 * hh + 4)], dma=f"hld{hh}")
            for li in range(NL):
                cur = free_slots.pop()
                prv = prev_slot[li]
                KTc, Vc = KTs[cur], Vs[cur]
                pre_norm(li, P_PREMIX)
                sl0 = take_A()
                sl1 = take_A()
                for tt in range(4):
                    bank = tt % 4
                    for k in range(KT):
                        sl = sl0 if k < 4 else sl1
                        rhs = ringA[:, sl, (k % 4) * 512:(k % 4 + 1) * 512]
                        A("pe", lambda e, bank=bank, k=k, tt=tt, rhs=rhs: e.matmul(
                            ps[bank][:, :], xn[:, k, tt * 128:(tt + 1) * 128], rhs, start=(k == 0), stop=(k == KT - 1)),
                          reads=[("ra", sl), ("xn", k)], writes=[("ps", bank)])
                    A("act", lambda e, bank=bank, tt=tt, Vc=Vc: e.activation(out=Vc[:, tt, :], in_=ps[bank][:, :],
                                                                            func=AF.Copy),
                      reads=[("ps", bank)], writes=[("v", cur, tt)])
                issue_A()
                issue_A()

                def evac_proj(m, bank, li=li, KTc=KTc, cur=cur):
                    t = m % 2
                    if m < 2:
                        A("act", lambda e: e.activation(out=gb_sb[:, t, :], in_=ps[bank][:, :], func=AF.Copy),
                          reads=[("ps", bank)], writes=K_gb[t])
                    elif m < 4:
                        A("act", lambda e: e.activation(out=gc_sb[:, t, :], in_=ps[bank][:, :], func=AF.Copy),
                          reads=[("ps", bank)], writes=K_gc[t])
                    elif m < 6:
                        A("dve", lambda e: e.tensor_tensor(out=zbuf[:, t, 2:514], in0=ps[bank][:, :], in1=gc_sb[:, t, :],
                                                           op=ALU.mult),
                          reads=[("ps", bank)] + K_gc[t], writes=K_z[t])
                    elif m < 8:
                        A("act", lambda e: e.activation(out=pu_sb[:, t, 16:528], in_=ps[bank][:, :], func=AF.Copy),
                          reads=[("ps", bank)], writes=K_pu[t])
                    elif m < 12:
                        tq = m - 8
                        A("dve", lambda e: e.tensor_copy(out=qT[:, tq, :], in_=ps[bank][:, :]),
                          reads=[("ps", bank)], writes=K_q[tq])
                    else:
                        tk = m - 12
                        A("act", lambda e: e.activation(out=KTc[:, tk, :], in_=ps[bank][:, :], func=AF.Copy),
                          reads=[("ps", bank)], writes=[("kt", cur, tk)])

                proj_chunks(16, [xn[:, k, :] for k in range(KT)], [[("xn", k)] for k in range(KT)], KT,
                            evac_proj, banks=[4, 5, 6, 0, 1, 2, 3])

                for t in range(2):
                    if b == 0:
                        A("dve", lambda e, t=t: e.memset(zbuf[:, t, 0:2], 0.0), writes=K_z[t])
                    else:
                        A("dve", lambda e, t=t, li=li: e.tensor_copy(out=zbuf[:, t, 0:2], in_=zc[:, li, t, :]),
                          reads=[("zc", li, t)], writes=K_z[t])
                    cw = lambda tap, t=t, li=li: prm[:, li, P_CONV + tap * 2 + t:P_CONV + tap * 2 + t + 1]
                    A("dve", lambda e, t=t, cw=cw: e.tensor_scalar(out=tmpP[:, 0, :], in0=zbuf[:, t, 2:514],
                                                                     scalar1=cw(2), scalar2=None, op0=ALU.mult),
                      reads=K_z[t] + ["prm"], writes=[("tmpP", 0)])
                    A("dve", lambda e, t=t, cw=cw: e.scalar_tensor_tensor(out=tmpP[:, 1, :], in0=zbuf[:, t, 1:513],
                                                                            scalar=cw(1), in1=tmpP[:, 0, :],
                                                                            op0=ALU.mult, op1=ALU.add),
                      reads=K_z[t] + [("tmpP", 0), "prm"], writes=[("tmpP", 1)])
                    A("dve", lambda e, t=t, cw=cw: e.scalar_tensor_tensor(out=tmpP[:, 0, :], in0=zbuf[:, t, 0:512],
                                                                            scalar=cw(0), in1=tmpP[:, 1, :],
                                                                            op0=ALU.mult, op1=ALU.add),
                      reads=K_z[t] + [("tmpP", 1), "prm"], writes=[("tmpP", 0)])
                    A("dve", lambda e, t=t: e.tensor_tensor(out=ya[:, t, :], in0=tmpP[:, 0, :], in1=gb_sb[:, t, :],
                                                             op=ALU.mult),
                      reads=[("tmpP", 0)] + K_gb[t], writes=K_ya[t])
                    A("dve", lambda e, t=t, li=li: e.tensor_copy(out=zc[:, li, t, :], in_=zbuf[:, t, 512:514]),
                      reads=K_z[t], writes=[("zc", li, t)])

                for t in range(2):
                    X = pu_sb[:, t, :]
                    if b == 0:
                        A("dve", lambda e, t=t: e.memset(pu_sb[:, t, 0:16], 0.0), writes=K_pu[t])
                    else:
                        A("dve", lambda e, t=t, li=li: e.tensor_copy(out=pu_sb[:, t, 0:16], in_=pc[:, li, t, :]),
                          reads=[("pc", li, t)], writes=K_pu[t])
                    W1, W2, W3 = wsT[:, 0, :], wsT[:, 1, :], wsT[:, 2, :]
                    A("dve", lambda e, X=X, W1=W1: e.tensor_tensor(out=W1[:, 1:528], in0=X[:, 1:528], in1=X[:, 0:527],
                                                                    op=ALU.add),
                      reads=K_pu[t], writes=[("ws", 0)])
                    if t == 0:
                        fin_lo, fin_hi = W1, W2
                        A("dve", lambda e, W1=W1, W2=W2: e.tensor_tensor(out=W2[64:128, 3:528], in0=W1[64:128, 3:528],
                                                                          in1=W1[64:128, 1:526], op=ALU.add),
                          reads=[("ws", 0)], writes=[("ws", 1)])
                        wl, wh = 2, 4
                        rk_lo, rk_hi = [("ws", 0)], [("ws", 1)]
                    else:
                        A("dve", lambda e, W1=W1, W2=W2: e.tensor_tensor(out=W2[:, 3:528], in0=W1[:, 3:528],
                                                                          in1=W1[:, 1:526], op=ALU.add),
                          reads=[("ws", 0)], writes=[("ws", 1)])
                        A("dve", lambda e, W2=W2, W3=W3: e.tensor_tensor(out=W3[:, 7:528], in0=W2[:, 7:528],
                                                                          in1=W2[:, 3:524], op=ALU.add),
                          reads=[("ws", 1)], writes=[("ws", 2)])
                        A("dve", lambda e, W1=W1, W3=W3: e.tensor_tensor(out=W1[64:128, 15:528], in0=W3[64:128, 15:528],
                                                                          in1=W3[64:128, 7:520], op=ALU.add),
                          reads=[("ws", 2), ("ws", 0)], writes=[("ws", 0)])
                        fin_lo, fin_hi = W3, W1
                        wl, wh = 8, 16
                        rk_lo, rk_hi = [("ws", 2)], [("ws", 0)]
                    for (p0, p1, fin, w, rk) in ((0, 64, fin_lo, wl, rk_lo), (64, 128, fin_hi, wh, rk_hi)):
                        A("dve", lambda e, p0=p0, p1=p1, fin=fin, w=w, t=t, X=X: e.scalar_tensor_tensor(
                            out=dbf[p0:p1, t, :], in0=fin[p0:p1, 16:528], scalar=1.0 / w, in1=X[p0:p1, 16:528],
                            op0=ALU.mult, op1=ALU.subtract),
                          reads=rk + K_pu[t], writes=[("dbf", t)])
                        if b == 0:
                            A("dve", lambda e, p0=p0, p1=p1, fin=fin, t=t: e.tensor_tensor(
                                out=tmpA[p0:p1, 0, 0:16], in0=fin[p0:p1, 16:32], in1=cst[p0:p1, 128 + t * 16:128 + t * 16 + 16],
                                op=ALU.mult),
                              reads=rk + ["cst"], writes=[("tmpA", 0)])
                            A("dve", lambda e, p0=p0, p1=p1, t=t, X=X: e.tensor_tensor(
                                out=dbf[p0:p1, t, 0:16], in0=tmpA[p0:p1, 0, 0:16], in1=X[p0:p1, 16:32], op=ALU.subtract),
                              reads=[("tmpA", 0)] + K_pu[t], writes=[("dbf", t)])
                    A("dve", lambda e, t=t, li=li, X=X: e.tensor_copy(out=pc[:, li, t, :], in_=X[:, 512:528]),
                      reads=K_pu[t], writes=[("pc", li, t)])

                scale = 0.125
                pitems = []
                for t in range(4):
                    kts = [("c", 0)] + ([("p", j) for j in range(4)] if prv is not None else []) + \
                          [("c", j) for j in range(1, 4)]
                    for ki, (kind, j) in enumerate(kts):
                        pitems.append((t, kind, j, ki == 0, ki == len(kts) - 1))
                rot = [0, 0]
                ptc = [0]
                OB, DB = 6, 7

                def next_spair():
                    r = rot[0] % 3
                    rot[0] += 1
                    return r

                def next_sbank(e2):
                    return 2 * next_spair() + e2

                def emit_qk_pair(pit, li=li, prv=prv, cur=cur):
                    t, kind, j, first, last = pit
                    pslot = ptc[0] % 3
                    ptc[0] += 1
                    if kind == "p":
                        qa, qb = 0, 128 * (j + 1)
                        Ksrc, kkey = KTs[prv], ("kt", prv, t)
                    else:
                        qa, qb = 128 * j, 512
                        Ksrc, kkey = KTs[cur], ("kt", cur, t)
                    nq = qb - qa
                    tblx = None
                    if kind == "p":
                        if j == 3:
                            tblx = (0, 128, 128)
                    else:
                        tblx = (0, min(256, nq), 0)
                    r = next_spair()
                    banks = [2 * r, 2 * r + 1]
                    for e2 in range(2):
                        p0, p1 = 64 * e2, 64 * e2 + 64
                        bank = banks[e2]
                        A("pe", lambda e, bank=bank, p0=p0, p1=p1: e.matmul(
                            ps[bank][:, 0:nq], Ksrc[p0:p1, t, 128 * j:128 * j + 128], qT[p0:p1, t, qa:qb],
                            start=True, stop=(tblx is None)),
                          reads=[kkey] + K_q[t], writes=[("ps", bank)])
                    if tblx is not None:
                        c0, c1, d0 = tblx
                        for e2 in range(2):
                            h = 2 * t + e2
                            bank = banks[e2]
                            A("pe", lambda e, bank=bank, h=h: e.matmul(
                                ps[bank][:, c0:c1], ident[:, :], tbl8[:, li, h, d0:d0 + (c1 - c0)],
                                start=False, stop=True),
                              reads=["ident", ("tbl8", li)], writes=[("ps", bank)])
                    Sv = psA[:, 2 * r * 512:(2 * r + 2) * 512].rearrange("p (e n) -> p e n", e=2)
                    Pv = PT[:, 2 * pslot:2 * pslot + 2, :]
                    rk = [("ps", banks[0]), ("ps", banks[1])]
                    wk = [("pt", 2 * pslot), ("pt", 2 * pslot + 1)]
                    if kind == "p":
                        A("act", lambda e: e.activation(out=Pv[:, :, 0:nq - 64], in_=Sv[:, :, 0:nq - 64],
                                                        func=AF.Exp, scale=scale),
                          reads=rk, writes=wk)
                        A("act", lambda e: e.activation(out=Pv[:, :, nq - 64:nq], in_=Sv[:, :, nq - 64:nq],
                                                        func=AF.Exp, scale=scale, bias=maskcol[:, 0:1]),
                          reads=rk + ["maskcol"], writes=wk)
                    else:
                        A("act", lambda e: e.activation(out=Pv[:, :, 0:nq], in_=Sv[:, :, 0:nq],
                                                        func=AF.Exp, scale=scale),
                          reads=rk, writes=wk)
                    return (pit, pslot, qa, qb, nq)

                def emit_pv_pair(rec, li=li, prv=prv, cur=cur):
                    (t, kind, j, first, last), pslot, qa, qb, nq = rec
                    if kind == "p":
                        Vsrc, vkey = Vs[prv], ("v", prv, j)
                    else:
                        Vsrc, vkey = Vs[cur], ("v", cur, j)
                    for e2 in range(2):
                        h = 2 * t + e2
                        p0, p1 = 64 * e2, 64 * e2 + 64
                        pb = 2 * pslot + e2
                        A("pe", lambda e, h=h, p0=p0, p1=p1, pb=pb: e.matmul(
                            ps[OB][p0:p1, qa:qb], Vsrc[:, j, h * 64:(h + 1) * 64], PT[:, pb, 0:nq],
                            start=first, stop=last, tile_position=(0, p0)),
                          reads=[vkey, ("pt", pb)], writes=[("ps", OB)])
                    for e2 in range(2):
                        p0, p1 = 64 * e2, 64 * e2 + 64
                        pb = 2 * pslot + e2
                        A("pe", lambda e, p0=p0, p1=p1, pb=pb: e.matmul(
                            ps[DB][p0:p1, qa:qb], ones[:, 0:64], PT[:, pb, 0:nq],
                            start=first, stop=last, tile_position=(0, p0)),
                          reads=["ones", ("pt", pb)], writes=[("ps", DB)])
                    if last:
                        A("dve", lambda e: e.tensor_copy(out=onb[:, t, :], in_=ps[OB][:, :]),
                          reads=[("ps", OB)], writes=K_on[t])
                        A("act", lambda e: e.activation(out=tmpP[:, 0, :], in_=ps[DB][:, :], func=AF.Ln),
                          reads=[("ps", DB)], writes=[("tmpP", 0)])
                        A("act", lambda e: e.activation(out=tmpP[:, 1, :], in_=tmpP[:, 0, :], func=AF.Exp, scale=-1.0),
                          reads=[("tmpP", 0)], writes=[("tmpP", 1)])
                        A("dve", lambda e: e.tensor_tensor(out=onb[:, t, :], in0=onb[:, t, :], in1=tmpP[:, 1, :],
                                                           op=ALU.mult),
                          reads=K_on[t] + [("tmpP", 1)], writes=K_on[t])
                        if t == 0:
                            A("dve", lambda e: e.tensor_tensor(out=sqacc[:, :], in0=onb[:, t, :], in1=onb[:, t, :],
                                                               op=ALU.mult),
                              reads=K_on[t], writes=["sqacc"])
                        else:
                            A("dve", lambda e: e.tensor_tensor(out=tmpA[:, 0, :], in0=onb[:, t, :], in1=onb[:, t, :],
                                                               op=ALU.mult),
                              reads=K_on[t], writes=[("tmpA", 0)])
                            if t < 3:
                                A("dve", lambda e: e.tensor_tensor(out=sqacc[:, :], in0=sqacc[:, :], in1=tmpA[:, 0, :],
                                                                   op=ALU.add),
                                  reads=["sqacc", ("tmpA", 0)], writes=["sqacc"])
                            else:
                                A("dve", lambda e: e.tensor_tensor(out=sqat[:, :], in0=sqacc[:, :], in1=tmpA[:, 0, :],
                                                                   op=ALU.add),
                                  reads=["sqacc", ("tmpA", 0)], writes=["sqat"])

                def pool_linear(li=li):
                    for t in range(2):
                        bank = next_sbank(t)
                        A("pe", lambda e, t=t, bank=bank: e.matmul(ps[bank][:, :], poolW[:, li, t * 128:(t + 1) * 128],
                                                                   dbf[:, t, :], start=True, stop=True),
                          reads=["poolW", ("dbf", t)], writes=[("ps", bank)])
                        A("act", lambda e, t=t, bank=bank: e.activation(out=ybuf[:, t, :], in_=ps[bank][:, :],
                                                                        func=AF.Copy,
                                                                        scale=prm[:, li, P_PSCALE + t:P_PSCALE + t + 1]),
                          reads=[("ps", bank), "prm"], writes=K_yb[t])

                def sq_only(ap, keys):
                    sb_i = sqc[0] % 4
                    sqc[0] += 1
                    A("dve", lambda e: e.tensor_tensor(out=sq[:, sb_i, :], in0=ap, in1=ap, op=ALU.mult),
                      reads=keys, writes=[("sq", sb_i)])
                    return sb_i

                def stats_mm(sbl, bank):
                    n = len(sbl)
                    for i, sb_i in enumerate(sbl):
                        A("pe", lambda e, i=i, sb_i=sb_i: e.matmul(ps[bank][:, :], ones[:, :], sq[:, sb_i, :],
                                                                   start=(i == 0), stop=(i == n - 1)),
                          reads=[("sq", sb_i), "ones"], writes=[("ps", bank)])

                nst = {}

                def conv_norm_1():
                    nst["conv"] = [sq_only(ya[:, t, :], K_ya[t]) for t in range(2)]

                def conv_norm_2(li=li):
                    bank = next_sbank(0)
                    stats_mm(nst["conv"], bank)
                    rstd_from(bank, 256, 1)
                    for t in range(2):
                        A("dve", lambda e, t=t: e.scalar_tensor_tensor(
                            out=yTb[:, t, :], in0=ya[:, t, :], scalar=prm[:, li, P_GG + t:P_GG + t + 1],
                            in1=rstd[:, 1, :], op0=ALU.mult, op1=ALU.mult),
                          reads=K_ya[t] + [("rstd", 1), "prm"], writes=K_y[t])

                def pool_norm_1():
                    nst["pool"] = [sq_only(ybuf[:, t, :], K_yb[t]) for t in range(2)]

                def pool_norm_2(li=li):
                    bank = next_sbank(1)
                    stats_mm(nst["pool"], bank)
                    rstd_from(bank, 256, 1)
                    for t in range(2):
                        A("dve", lambda e, t=t: e.scalar_tensor_tensor(
                            out=yTb[:, 2 + t, :], in0=ybuf[:, t, :], scalar=prm[:, li, P_GG + 2 + t:P_GG + 3 + t],
                            in1=rstd[:, 1, :], op0=ALU.mult, op1=ALU.mult),
                          reads=K_yb[t] + [("rstd", 1), "prm"], writes=K_y[2 + t])

                npp = len(pitems) // 4
                LAGP = 2
                i1 = npp + 2
                i2 = i1 + max(2, npp // 2)
                i3 = i2 + max(2, npp // 2)
                pend = []
                for idx, pit in enumerate(pitems):
                    if idx == i1:
                        conv_norm_1()
                        pool_linear()
                    if idx == i2:
                        conv_norm_2()
                        pool_norm_1()
                    if idx == i3:
                        pool_norm_2()
                    pend.append(emit_qk_pair(pit))
                    if len(pend) > LAGP:
                        emit_pv_pair(pend.pop(0))
                while pend:
                    emit_pv_pair(pend.pop(0))

                A("pe", lambda e: e.matmul(ps[0][:, :], ones[:, :], sqat[:, :], start=True, stop=True),
                  reads=["sqat", "ones"], writes=[("ps", 0)])
                rstd_from(0, 512, 1)
                for t in range(4):
                    A("dve", lambda e, t=t, li=li: e.scalar_tensor_tensor(
                        out=yTb[:, 4 + t, :], in0=onb[:, t, :], scalar=prm[:, li, P_GG + 4 + t:P_GG + 5 + t],
                        in1=rstd[:, 1, :], op0=ALU.mult, op1=ALU.mult),
                      reads=K_on[t] + [("rstd", 1), "prm"], writes=K_y[4 + t])

                if prv is not None:
                    free_slots.append(prv)
                prev_slot[li] = cur

                wo_slots = [take_A(), take_A()]
                wo_banks = [1, 2, 4, 5]
                for half in range(2):
                    for ci in range(4):
                        slot = wo_slots[ci // 2]
                        sidx = ci % 2
                        bank = wo_banks[ci]
                        for k in range(4 * half, 4 * half + 4):
                            lhsT = ringA[:, slot, sidx * 1024 + k * 128: sidx * 1024 + (k + 1) * 128]
                            A("pe", lambda e, lhsT=lhsT, k=k, bank=bank: e.matmul(
                                ps[bank][:, :], lhsT, yTb[:, k, :], start=(k == 0), stop=(k == KT - 1)),
                              reads=[("ra", slot)] + K_y[k], writes=[("ps", bank)])
                wo_def = []
                for ci in range(4):
                    wo_def.append(evac_branch(ci, wo_banks[ci]))
                issue_A()
                issue_A()
                wo_bi = 0
                for pr in range(2):
                    slot = take_A()
                    for sidx in range(2):
                        m = 4 + 2 * pr + sidx
                        bank = wo_banks[wo_bi % 4]
                        wo_bi += 1
                        for k in range(KT):
                            lhsT = ringA[:, slot, sidx * 1024 + k * 128: sidx * 1024 + (k + 1) * 128]
                            A("pe", lambda e, lhsT=lhsT, k=k, bank=bank: e.matmul(
                                ps[bank][:, :], lhsT, yTb[:, k, :], start=(k == 0), stop=(k == KT - 1)),
                              reads=[("ra", slot)] + K_y[k], writes=[("ps", bank)])
                        while len(wo_def) > 2:
                            wo_def.pop(0)()
                        wo_def.append(evac_branch(m, bank))
                    issue_A()
                for dfr in wo_def:
                    dfr()
                post_norm_residual(li, P_POSTMIX)

                pre_norm(li, P_PREFFN)
                gu_state = {}

                def evac_gu(m, bank):
                    f = m // 2
                    if m % 2 == 0:
                        ti = f % 2
                        gu_state["ti"] = ti
                        A("act", lambda e: e.activation(out=tmpA[:, ti, :], in_=ps[bank][:, :], func=AF.Silu),
                          reads=[("ps", bank)], writes=[("tmpA", ti)])
                    else:
                        ti = gu_state["ti"]
                        A("dve", lambda e: e.tensor_tensor(out=actT[:, f, :], in0=ps[bank][:, :], in1=tmpA[:, ti, :],
                                                           op=ALU.mult),
                          reads=[("ps", bank), ("tmpA", ti)], writes=K_act[f])

                proj_chunks(44, [xn[:, k, :] for k in range(KT)], [[("xn", k)] for k in range(KT)], KT,
                            evac_gu, banks=[0, 1, 2, 3, 4, 5])
                A("act", lambda e: e.activation(out=dummy[:, 0:1], in_=epsT[:, 0:1], func=AF.Ln),
                  reads=["epsT"], writes=["dummy"])
                proj_chunks(8, [actT[:, f, :] for f in range(FT)], [K_act[f] for f in range(FT)], FT,
                            evac_branch, banks=[0, 1, 2, 3], ring="B")
                post_norm_residual(li, P_POSTFFN)

            for hh in range(2):
                A("sp", lambda e, b=b, hh=hh: e.dma_start(out=yT_v[:, 4 * hh:4 * hh + 4, b * TB:(b + 1) * TB],
                                                          in_=hT[:, 4 * hh:4 * hh + 4, :]),
                  reads=[("h", k) for k in range(4 * hh, 4 * hh + 4)], dma=f"yst{hh}")

        sch.finalize()
        n_out = sch.dma_count.get("yst0", 0)
        block = es.enter_context(nc.Block())

        def emit(e, name):
            for op in sch.ops[name]:
                for (kind, key), v in op.waits:
                    e.wait_ge(sem_dma[key] if kind == "dma" else sem_eng[key], v)
                ins = op.fn(e)
                if op.dma is not None:
                    ins.then_inc(sem_dma[op.dma], 16)
                elif op.marked:
                    ins.then_inc(sem_eng[name], 1)

        @block.tensor
        def _(e):
            emit(e, "pe")

        @block.scalar
        def _(e):
            emit(e, "act")

        @block.vector
        def _(e):
            emit(e, "dve")

        @block.gpsimd
        def _(e):
            emit(e, "pool")

        @block.sync
        def _(e):
            emit(e, "sp")
            e.wait_ge(sem_dma["yst0"], 16 * n_out)
            e.wait_ge(sem_dma["yst1"], 16 * n_out)

    counts = {k: len(v) for k, v in sch.ops.items()}
    return nc, counts


def _chunks(W):
    K, N = W.shape
    return np.ascontiguousarray(W.reshape(K // 128, 128, N // 128, 128).transpose(2, 1, 0, 3)).reshape(
        N // 128, 128, K)


def prep_layer_flat(w_in, w_out, w_gu, w_dn):
    V = w_in[:, 2048:2560].reshape(8, 128, 512).transpose(1, 0, 2).reshape(128, 4096)
    win = _chunks(w_in[:, 0:2048]).transpose(1, 0, 2).reshape(128, -1)
    wout = _chunks(w_out).transpose(1, 0, 2).reshape(128, -1)
    g = _chunks(w_gu[:, 0:DFF])
    u = _chunks(w_gu[:, DFF:])
    gu = np.stack([g, u], axis=1).reshape(44, 128, 1024).transpose(1, 0, 2).reshape(128, -1)
    dn = _chunks(w_dn).transpose(1, 0, 2).reshape(128, -1)
    flat = np.concatenate([V, win, wout, gu, dn], axis=1)
    assert flat.shape == (128, WROW)
    return np.ascontiguousarray(flat, dtype=np.float32)


def prep_params(conv_w, pool_scale, rel_bias, group_gain, pre_mix_g, post_mix_g, pre_ffn_g, post_ffn_g):
    prm = np.zeros((128, NPRM), np.float32)
    prm[:, P_PREMIX:P_PREMIX + 8] = pre_mix_g.reshape(8, 128).T
    prm[:, P_POSTMIX:P_POSTMIX + 8] = post_mix_g.reshape(8, 128).T
    prm[:, P_PREFFN:P_PREFFN + 8] = pre_ffn_g.reshape(8, 128).T
    prm[:, P_POSTFFN:P_POSTFFN + 8] = post_ffn_g.reshape(8, 128).T
    prm[:, P_GG:P_GG + 8] = group_gain.reshape(8, 128).T
    for tap in range(3):
        prm[:, P_CONV + tap * 2:P_CONV + tap * 2 + 2] = conv_w[tap].reshape(2, 128).T
    prm[:, P_PSCALE:P_PSCALE + 2] = pool_scale.reshape(2, 128).T
    prm[:, P_CH:P_CH + 8] = np.broadcast_to(rel_bias[:, 256][None, :], (128, 8))
    kl = np.arange(128)[:, None]
    d = np.arange(256)[None, :]
    idx = np.clip(d - kl, -128, 128) + 128
    tbl = rel_bias[:, idx]
    tbl = np.ascontiguousarray(tbl.transpose(1, 0, 2)).reshape(128, NHEAD * 256)
    return prm, tbl


def prep_poolw(pool_w):
    bd = np.zeros((128, 256), np.float32)
    for t in range(2):
        for gsub in range(2):
            g = 2 * t + gsub
            bd[gsub * 64:(gsub + 1) * 64, t * 128 + gsub * 64:t * 128 + (gsub + 1) * 64] = pool_w[g]
    return bd


def prep_cst():
    cst = np.zeros((128, 160), np.float32)
    cst[:, 0:128] = np.eye(128, dtype=np.float32)
    wins = (2, 4, 8, 16)
    tpos = np.arange(16)
    for t in range(2):
        for gsub in range(2):
            w = wins[2 * t + gsub]
            cst[gsub * 64:(gsub + 1) * 64, 128 + t * 16:128 + t * 16 + 16] = 1.0 / np.minimum(tpos + 1, w)
    return cst


def make_shared_inputs(inputs, layers):
    f = lambda k: np.asarray(inputs[k], dtype=np.float32)
    w_in, w_out, w_gu, w_dn = f("w_in"), f("w_out"), f("w_gate_up"), f("w_down")
    wflat = np.stack([prep_layer_flat(w_in[l], w_out[l], w_gu[l], w_dn[l]) for l in layers])
    prms, tbls, pws = [], [], []
    for l in layers:
        p, t = prep_params(f("conv_w")[l], f("pool_scale")[l], f("rel_bias")[l], f("group_gain")[l],
                           f("pre_mix_g")[l], f("post_mix_g")[l], f("pre_ffn_g")[l], f("post_ffn_g")[l])
        prms.append(p)
        tbls.append(t)
        pws.append(prep_poolw(f("pool_w")[l]))
    return {"wflat": wflat, "prm": np.stack(prms), "tbl": np.stack(tbls), "poolw": np.stack(pws),
            "cst": prep_cst()}


_CACHE = {}


def kernel(x, w_in, w_out, conv_w, pool_w, pool_scale, rel_bias, group_gain,
           pre_mix_g, post_mix_g, pre_ffn_g, post_ffn_g, w_gate_up, w_down):
    inputs = dict(x=x, w_in=w_in, w_out=w_out, conv_w=conv_w, pool_w=pool_w, pool_scale=pool_scale,
                  rel_bias=rel_bias, group_gain=group_gain, pre_mix_g=pre_mix_g, post_mix_g=post_mix_g,
                  pre_ffn_g=pre_ffn_g, post_ffn_g=post_ffn_g, w_gate_up=w_gate_up, w_down=w_down)
    x = np.asarray(x, dtype=np.float32)
    B, S, _ = x.shape
    NB = S // TB
    layers = tuple(range(DEPTH))
    key = (NB, layers)
    if key not in _CACHE:
        _CACHE[key] = build_program(NB, layers)[0]
    nc = _CACHE[key]
    shared = make_shared_inputs(inputs, layers)
    in_maps = []
    for bi in range(B):
        m = dict(shared)
        m["xT"] = np.ascontiguousarray(x[bi].T)
        in_maps.append(m)
    res = run_bass_kernel_spmd(nc, in_maps, core_ids=list(range(B)))
    out = np.stack([np.ascontiguousarray(r["yT"].T) for r in res.results], axis=0)
    return out.astype(np.float32)
```
